# Optimizing a Trainium2 kernel written in Bass

```python
import math
import jax
import jax.numpy as jnp
from jax import lax
import numpy as np

D_MODEL = 2048
BATCH = 4
SEQ = 4096
DEPTH = 2

CHUNK = 64
Q_BLOCK = 128
EPS = 1e-6
N_BRANCH = 3
BRANCH_WIDTH = D_MODEL // 2
A_NOPE = 128
A_ROPE = 64
A_V = 128
A_HEADS = BRANCH_WIDTH // A_V
A_Q_RANK = D_MODEL // 4
A_KV_RANK = D_MODEL // 8
A_ROPE_THETA = 10000.0
B_DK = 64
B_DV = 128
B_HEADS = BRANCH_WIDTH // B_DV
B_ROT = B_DK // 4
B_ROPE_THETA = 500000.0
C_DH = 128
C_HEADS = BRANCH_WIDTH // C_DH
C_LEFT_CHUNKS = 8
C_BAND = (C_LEFT_CHUNKS + 1) * CHUNK
C_REL_MAX = 128
C_N_REL = (CHUNK - 1) + C_REL_MAX + 1

IN_SPLITS = (A_Q_RANK, A_KV_RANK, A_ROPE, BRANCH_WIDTH,
             B_HEADS * 2 * B_DK, B_HEADS * 2 * B_DK, B_HEADS * B_DV, BRANCH_WIDTH,
             C_HEADS * C_DH, C_HEADS * C_DH, C_HEADS * C_DH, BRANCH_WIDTH,
             N_BRANCH * D_MODEL)
D_IN = sum(IN_SPLITS)
SPLIT_POINTS = tuple(int(v) for v in np.cumsum(IN_SPLITS)[:-1])

kernel_name = "hybrid_mla_diff_chunkband_block"


def _rmsnorm(x, g):
    xf = x.astype(jnp.float32)
    y = xf * lax.rsqrt(jnp.mean(xf * xf, axis=-1, keepdims=True) + EPS)
    return (y * g.astype(jnp.float32)).astype(x.dtype)


def _rope_tables(s, dim, theta):
    inv = 1.0 / (jnp.float32(theta) ** (jnp.arange(0, dim, 2, dtype=jnp.float32) / dim))
    ang = jnp.arange(s, dtype=jnp.float32)[:, None] * inv[None, :]
    return jnp.cos(ang), jnp.sin(ang)


def _rope(x, cos, sin):
    half = x.shape[-1] // 2
    shape = (1, x.shape[1]) + (1,) * (x.ndim - 3) + (half,)
    c = cos.reshape(shape).astype(x.dtype)
    sn = sin.reshape(shape).astype(x.dtype)
    x1, x2 = x[..., :half], x[..., half:]
    return jnp.concatenate([x1 * c - x2 * sn, x2 * c + x1 * sn], axis=-1)


def _partial_rope(x, cos, sin):
    return jnp.concatenate([_rope(x[..., :B_ROT], cos, sin), x[..., B_ROT:]], axis=-1)


def _to_blocks(t, size):
    b, s = t.shape[0], t.shape[1]
    return jnp.moveaxis(t.reshape((b, s // size, size) + t.shape[2:]), 1, 0)


def _from_blocks(t):
    t = jnp.moveaxis(t, 0, 1)
    return t.reshape((t.shape[0], t.shape[1] * t.shape[2]) + t.shape[3:])


def _chunk_causal_mask(q_start, q_len, k_len):
    q_chunk = (q_start + jnp.arange(q_len)) // CHUNK
    k_chunk = jnp.arange(k_len) // CHUNK
    return k_chunk[None, :] <= q_chunk[:, None]


def _mla_attention(qn, qr, kn, kr, v):
    s_len = kn.shape[1]
    scale = (A_NOPE + A_ROPE) ** -0.5

    def block(args):
        qn_b, qr_b, i = args
        sc = (jnp.einsum('bqhd,bkhd->bhqk', qn_b, kn)
              + jnp.einsum('bqhr,bkr->bhqk', qr_b, kr)).astype(jnp.float32) * scale
        mask = _chunk_causal_mask(i * Q_BLOCK, Q_BLOCK, s_len)
        p = jax.nn.softmax(jnp.where(mask, sc, -jnp.inf), axis=-1)
        return jnp.einsum('bhqk,bkhd->bqhd', p.astype(v.dtype), v)

    n_blocks = s_len // Q_BLOCK
    out = lax.map(block, (_to_blocks(qn, Q_BLOCK), _to_blocks(qr, Q_BLOCK), jnp.arange(n_blocks)))
    return _from_blocks(out)


def _diff_attention(q, k, v, lam):
    s_len = k.shape[1]
    scale = B_DK ** -0.5

    def block(args):
        q_b, i = args
        sc = jnp.einsum('bqhcd,bkhcd->bhcqk', q_b, k).astype(jnp.float32) * scale
        mask = _chunk_causal_mask(i * Q_BLOCK, Q_BLOCK, s_len)
        p = jax.nn.softmax(jnp.where(mask, sc, -jnp.inf), axis=-1)
        a = p[:, :, 0] - lam * p[:, :, 1]
        return jnp.einsum('bhqk,bkhe->bqhe', a.astype(v.dtype), v)

    n_blocks = s_len // Q_BLOCK
    out = lax.map(block, (_to_blocks(q, Q_BLOCK), jnp.arange(n_blocks)))
    return _from_blocks(out)


def _chunk_band_attention(q, k, v, rel_bias):
    s_len = q.shape[1]
    pad = C_LEFT_CHUNKS * CHUNK
    widths = ((0, 0), (pad, 0), (0, 0), (0, 0))
    k_pad = jnp.pad(k, widths)
    v_pad = jnp.pad(v, widths)
    q_idx = jnp.arange(CHUNK)[:, None]
    k_idx = jnp.arange(C_BAND)[None, :]
    rel = jnp.clip(pad + q_idx - k_idx, -(CHUNK - 1), C_REL_MAX) + (CHUNK - 1)
    bias = rel_bias.astype(jnp.float32)[:, rel]
    key_offset = jnp.arange(C_BAND) - pad
    scale = C_DH ** -0.5

    def one_chunk(c):
        start = c * CHUNK
        q_c = lax.dynamic_slice_in_dim(q, start, CHUNK, axis=1)
        k_c = lax.dynamic_slice_in_dim(k_pad, start, C_BAND, axis=1)
        v_c = lax.dynamic_slice_in_dim(v_pad, start, C_BAND, axis=1)
        sc = jnp.einsum('bqhd,bkhd->bhqk', q_c, k_c).astype(jnp.float32) * scale + bias
        valid = (start + key_offset) >= 0
        p = jax.nn.softmax(jnp.where(valid, sc, -jnp.inf), axis=-1)
        return jnp.einsum('bhqk,bkhd->bqhd', p.astype(v.dtype), v_c)

    out = lax.map(one_chunk, jnp.arange(s_len // CHUNK))
    return _from_blocks(out)


def _hybrid_layer(x, layer_idx, rope_a, rope_b, g_pre, w_in, a_g_cq, a_g_ckv, a_w_uq, a_w_ukv,
                  a_g_q, a_g_k, b_g_q, b_g_k, b_lam, b_g_sub, c_g_q, c_g_k, c_rel_bias,
                  w_branch, w_out):
    b, s, _ = x.shape
    h = _rmsnorm(x, g_pre)
    u = h @ w_in
    (a_cq, a_ckv, a_kr, a_z, b_q, b_k, b_v, b_z,
     c_q, c_k, c_v, c_z, gate_logits) = jnp.split(u, SPLIT_POINTS, axis=-1)

    qa = (_rmsnorm(a_cq, a_g_cq) @ a_w_uq).reshape(b, s, A_HEADS, A_NOPE + A_ROPE)
    kva = (_rmsnorm(a_ckv, a_g_ckv) @ a_w_ukv).reshape(b, s, A_HEADS, A_NOPE + A_V)
    qn = _rmsnorm(qa[..., :A_NOPE], a_g_q[:A_NOPE])
    qr = _rope(_rmsnorm(qa[..., A_NOPE:], a_g_q[A_NOPE:]), *rope_a)
    kn = _rmsnorm(kva[..., :A_NOPE], a_g_k[:A_NOPE])
    kr = _rope(_rmsnorm(a_kr, a_g_k[A_NOPE:]), *rope_a)
    o_a = _mla_attention(qn, qr, kn, kr, kva[..., A_NOPE:]).reshape(b, s, BRANCH_WIDTH)

    bq = _partial_rope(_rmsnorm(b_q.reshape(b, s, B_HEADS, 2, B_DK), b_g_q), *rope_b)
    bk = _partial_rope(_rmsnorm(b_k.reshape(b, s, B_HEADS, 2, B_DK), b_g_k), *rope_b)
    bv = b_v.reshape(b, s, B_HEADS, B_DV)
    lam_init = 0.8 - 0.6 * math.exp(-0.3 * layer_idx)
    lf = b_lam.astype(jnp.float32)
    lam = jnp.exp(jnp.sum(lf[0] * lf[1])) - jnp.exp(jnp.sum(lf[2] * lf[3])) + lam_init
    ob = _rmsnorm(_diff_attention(bq, bk, bv, lam), b_g_sub) * (1.0 - lam_init)
    o_b = ob.reshape(b, s, BRANCH_WIDTH)

    cq = _rmsnorm(c_q.reshape(b, s, C_HEADS, C_DH), c_g_q)
    ck = _rmsnorm(c_k.reshape(b, s, C_HEADS, C_DH), c_g_k)
    cv = c_v.reshape(b, s, C_HEADS, C_DH)
    o_c = _chunk_band_attention(cq, ck, cv, c_rel_bias).reshape(b, s, BRANCH_WIDTH)

    gates = jax.nn.sigmoid(gate_logits.astype(jnp.float32)).astype(x.dtype).reshape(b, s, N_BRANCH, D_MODEL)
    branches = (o_a * jax.nn.silu(a_z), o_b * jax.nn.silu(b_z), o_c * jax.nn.silu(c_z))
    y = gates[:, :, 0] * (branches[0] @ w_branch[0])
    for n in range(1, N_BRANCH):
        y = y + gates[:, :, n] * (branches[n] @ w_branch[n])
    return y @ w_out


def setup_inputs(seed: int = 0) -> dict:
    key = jax.random.key(seed)
    ks = jax.random.split(key, 18)

    def nrm(k, shape, scale):
        return jax.random.normal(k, shape, jnp.float32) * scale

    def gain(k, shape):
        return 1.0 + 0.02 * jax.random.normal(k, shape, jnp.float32)

    return {
        "x": nrm(ks[0], (BATCH, SEQ, D_MODEL), 1.0),
        "g_pre": gain(ks[1], (DEPTH, D_MODEL)),
        "w_in": nrm(ks[2], (DEPTH, D_MODEL, D_IN), D_MODEL ** -0.5),
        "a_g_cq": gain(ks[3], (DEPTH, A_Q_RANK)),
        "a_g_ckv": gain(ks[4], (DEPTH, A_KV_RANK)),
        "a_w_uq": nrm(ks[5], (DEPTH, A_Q_RANK, A_HEADS * (A_NOPE + A_ROPE)), A_Q_RANK ** -0.5),
        "a_w_ukv": nrm(ks[6], (DEPTH, A_KV_RANK, A_HEADS * (A_NOPE + A_V)), A_KV_RANK ** -0.5),
        "a_g_q": gain(ks[7], (DEPTH, A_NOPE + A_ROPE)),
        "a_g_k": gain(ks[8], (DEPTH, A_NOPE + A_ROPE)),
        "b_g_q": gain(ks[9], (DEPTH, B_DK)),
        "b_g_k": gain(ks[10], (DEPTH, B_DK)),
        "b_lam": nrm(ks[11], (DEPTH, 4, B_DK), 0.1),
        "b_g_sub": gain(ks[12], (DEPTH, B_DV)),
        "c_g_q": gain(ks[13], (DEPTH, C_DH)),
        "c_g_k": gain(ks[14], (DEPTH, C_DH)),
        "c_rel_bias": nrm(ks[15], (DEPTH, C_HEADS, C_N_REL), 0.2),
        "w_branch": nrm(ks[16], (DEPTH, N_BRANCH, BRANCH_WIDTH, D_MODEL), BRANCH_WIDTH ** -0.5),
        "w_out": nrm(ks[17], (DEPTH, D_MODEL, D_MODEL), D_MODEL ** -0.5),
    }


def reference(x, g_pre, w_in, a_g_cq, a_g_ckv, a_w_uq, a_w_ukv, a_g_q, a_g_k, b_g_q, b_g_k,
              b_lam, b_g_sub, c_g_q, c_g_k, c_rel_bias, w_branch, w_out):
    s = x.shape[1]
    rope_a = _rope_tables(s, A_ROPE, A_ROPE_THETA)
    rope_b = _rope_tables(s, B_ROT, B_ROPE_THETA)
    for l in range(DEPTH):
        x = x + _hybrid_layer(x, l, rope_a, rope_b, g_pre[l], w_in[l], a_g_cq[l], a_g_ckv[l],
                              a_w_uq[l], a_w_ukv[l], a_g_q[l], a_g_k[l], b_g_q[l], b_g_k[l],
                              b_lam[l], b_g_sub[l], c_g_q[l], c_g_k[l], c_rel_bias[l],
                              w_branch[l], w_out[l])
    return x
```

```python
import contextlib
import math
import numpy as np
import ml_dtypes
import concourse.bass as bass
import concourse.mybir as mybir
from concourse.bass_utils import run_bass_kernel_spmd

F32 = mybir.dt.float32
BF16 = mybir.dt.bfloat16
AF = mybir.ActivationFunctionType
ALU = mybir.AluOpType
AX = mybir.AxisListType

SEQ = 4096
BATCH = 4
DEPTH = 2
D = 2048
D_IN = 16192
EPS = 1e-6
NCORES = 8
FUSED = True

G_PRE, G_CQ, G_CKV, G_AQ, G_AK, G_BQ, G_BK, G_BSUB, G_CQ2, G_CK, G_LAM, G_LAMC = (
    0, 2048, 2560, 2816, 3008, 3200, 3264, 3328, 3456, 3584, 3712, 3968)
GTOT = 3972
KROWS = 3136
KR_A, KR_AR, KR_B, KR_C = 0, 1024, 1088, 2112


class Buf:
    __slots__ = ("w", "r", "multi", "wl")

    def __init__(self, multi=False):
        self.w = None
        self.r = []
        self.multi = multi
        self.wl = []


class Op:
    __slots__ = ("eng", "idx", "fn", "waits", "inc", "ticket", "dma")

    def __init__(self, eng, idx, fn):
        self.eng = eng
        self.idx = idx
        self.fn = fn
        self.waits = []
        self.inc = False
        self.ticket = 0
        self.dma = None


class Sched:
    N_DMA_SEM = 48

    def __init__(self, nc, stack):
        self.nc = nc
        self.names = ["pe", "act", "dve", "pool", "sp"]
        self.ops = {k: [] for k in self.names}
        self.seen = {k: {} for k in self.names}
        self.esem = {k: stack.enter_context(nc.semaphore("sem_" + k)) for k in self.names}
        self.N_CC_SEM = 4
        self.dsem = [stack.enter_context(nc.semaphore("dsem%d" % i)) for i in range(self.N_DMA_SEM + self.N_CC_SEM)]
        self.dval = [0] * (self.N_DMA_SEM + self.N_CC_SEM)
        self.dnext = 0
        self.cnext = 0

    def _need(self, o, tok):
        seen = self.seen[o.eng]
        if tok[0] == "eng":
            _, name, p = tok
            if seen.get(name, -1) >= p.idx:
                return
            seen[name] = p.idx
            p.inc = True
            o.waits.append(tok)
        else:
            _, s, v = tok
            if seen.get(("d", s), 0) >= v:
                return
            seen[("d", s)] = v
            o.waits.append(tok)

    def op(self, eng, fn, reads=(), writes=(), dma=False, coll=False):
        lst = self.ops[eng]
        o = Op(eng, len(lst), fn)
        dma = dma or coll
        deps = []
        for b in reads:
            if b.multi:
                for t in reversed(b.wl):
                    deps.append((t, True))
            elif b.w is not None:
                deps.append((b.w, True))
        for b in writes:
            if b.multi:
                if b.r:
                    for t in b.r:
                        deps.append((t, False))
                    b.r = []
                    b.wl = []
                continue
            if b.w is not None:
                deps.append((b.w, False))
            for t in b.r:
                deps.append((t, False))
        for tok, raw in deps:
            if tok[0] == "eng" and tok[1] == eng and not dma and not raw:
                continue
            self._need(o, tok)
        if dma:
            if coll:
                s = self.N_DMA_SEM + self.cnext
                self.cnext = (self.cnext + 1) % self.N_CC_SEM
                step = 1
            else:
                s = self.dnext
                self.dnext = (self.dnext + 1) % self.N_DMA_SEM
                step = 16
            if self.dval[s] > 0:
                self._need(o, ("dma", s, self.dval[s]))
            self.dval[s] += step
            o.dma = (s, self.dval[s], step)
            me = ("dma", s, self.dval[s])
        else:
            me = ("eng", eng, o)
        lst.append(o)
        for b in reads:
            if me[0] == "eng":
                b.r = [t for t in b.r if not (t[0] == "eng" and t[1] == eng)]
            b.r.append(me)
        for b in writes:
            if b.multi:
                b.wl.append(me)
                continue
            b.w = me
            b.r = []
        return o

    def dma(self, q, out, in_, reads=(), writes=()):
        return self.op(q, lambda e: e.dma_start(out=out, in_=in_), reads, writes, dma=True)

    def barrier(self):
        last = {}
        for n in self.names:
            for o in reversed(self.ops[n]):
                if o.fn is not None and o.dma is None:
                    last[n] = o
                    break
        for n in self.names:
            o = Op(n, len(self.ops[n]), None)
            for m, p in last.items():
                if m != n:
                    self._need(o, ("eng", m, p))
            for s in range(len(self.dval)):
                if self.dval[s] > 0:
                    self._need(o, ("dma", s, self.dval[s]))
            self.ops[n].append(o)

    def emit(self):
        nc = self.nc
        for name in self.names:
            c = 0
            for o in self.ops[name]:
                if o.inc:
                    c += 1
                    o.ticket = c

        def run(name, e):
            sem = self.esem[name]
            for o in self.ops[name]:
                for tok in o.waits:
                    if tok[0] == "eng":
                        e.wait_ge(self.esem[tok[1]], tok[2].ticket)
                    else:
                        e.wait_ge(self.dsem[tok[1]], tok[2])
                if o.fn is None:
                    continue
                ins = o.fn(e)
                if o.dma is not None:
                    ins.then_inc(self.dsem[o.dma[0]], o.dma[2])
                elif o.inc:
                    ins.then_inc(sem, 1)

        with nc.Block() as block:
            @block.tensor
            def _(e):
                run("pe", e)

            @block.scalar
            def _(e):
                run("act", e)

            @block.vector
            def _(e):
                run("dve", e)

            @block.gpsimd
            def _(e):
                run("pool", e)

            @block.sync
            def _(e):
                run("sp", e)


class Ring:
    def __init__(self, tiles):
        self.t = list(tiles)
        self.b = [Buf() for _ in self.t]
        self.i = 0

    def next(self):
        i = self.i
        self.i = (i + 1) % len(self.t)
        return self.t[i], self.b[i]


class Prog:
    def __init__(self, mode, nlayers, NT):
        self.mode = mode
        self.L = nlayers
        self.NT = NT
        self.NTT = NT // 128
        self.NQB = NT // 512
        self.nc = bass.Bass("TRN2", target_bir_lowering=False)
        self.build()

    def ACT(self, out, in_, func, reads, writes, **kw):
        self.S.op("act", lambda e: e.activation(out=out, in_=in_, func=func, **kw), reads, writes)

    def TT(self, eng, out, in0, in1, op, reads, writes):
        self.S.op(eng, lambda e: e.tensor_tensor(out=out, in0=in0, in1=in1, op=op), reads, writes)

    def TS(self, eng, out, in0, s1, s2, op0, op1, reads, writes):
        if op1 is None:
            self.S.op(eng, lambda e: e.tensor_scalar(out=out, in0=in0, scalar1=s1, scalar2=None, op0=op0),
                      reads, writes)
        else:
            self.S.op(eng, lambda e: e.tensor_scalar(out=out, in0=in0, scalar1=s1, scalar2=s2, op0=op0, op1=op1),
                      reads, writes)

    def STT(self, out, in0, scalar, in1, op0, op1, reads, writes):
        self.S.op("dve", lambda e: e.scalar_tensor_tensor(out=out, in0=in0, scalar=scalar, in1=in1,
                                                         op0=op0, op1=op1), reads, writes)

    def CP(self, eng, out, in_, reads, writes):
        if eng == "act":
            self.S.op("act", lambda e: e.copy(out=out, in_=in_), reads, writes)
        else:
            self.S.op(eng, lambda e: e.tensor_copy(out=out, in_=in_), reads, writes)

    def RCP(self, out, in_, reads, writes):
        self.S.op("dve", lambda e: e.reciprocal(out=out, in_=in_), reads, writes)

    def RED(self, out, in_, reads, writes):
        self.S.op("dve", lambda e: e.tensor_reduce(out=out, in_=in_, axis=AX.X, op=ALU.add), reads, writes)

    def MM(self, out, lhsT, rhs, start, stop, reads, writes, sgc=False):
        self.S.op("pe", lambda e: e.matmul(out, lhsT, rhs, start=start, stop=stop, skip_group_check=sgc),
                  reads, writes)

    def TR(self, out, in_, reads, writes):
        ident = self.ident
        self.S.op("pe", lambda e: e.transpose(out, in_, ident[:]), list(reads) + [self.b_const], writes)

    def MEMSET(self, eng, ap, val, writes):
        self.S.op(eng, lambda e: e.memset(ap, val), (), writes)

    def sb(self, st, name, shape, dt):
        self._n += 1
        return st.enter_context(self.nc.sbuf_tensor("%s_%d" % (name, self._n), shape, dt))

    def build(self):
        nc, L, NT, NTT = self.nc, self.L, self.NT, self.NTT
        mode = self.mode
        self._n = 0
        dr = lambda name, shape, dt, kind: nc.dram_tensor(name, shape, dt, kind=kind).ap()
        EI, EO, IN = "ExternalInput", "ExternalOutput", "Internal"
        self.x_in = dr("x", [NT, D], F32, EI)
        self.w_in = dr("w_in", [L, D, D_IN], F32, EI)
        self.w_ukv = dr("w_ukv", [L, 256, 2048], F32, EI)
        self.gbd = dr("gb", [L, 128, GTOT], F32, EI)
        self.ropeA_d = dr("ropeA", [NT, 64], F32, EI)
        self.ropeB_d = dr("ropeB", [NT, 16], F32, EI)
        self.ident_d = dr("ident", [128, 128], F32, EI)
        if mode != "kv":
            self.w_uq = dr("w_uq", [L, 512, 1536], F32, EI)
            self.w_br = dr("w_br", [L, 3, 1024, 2048], F32, EI)
            self.w_out = dr("w_out", [L, D, D], F32, EI)
            self.m4_d = dr("m4", [4, 128, 512], F32, EI)
            self.mc_d = dr("mc", [8, 128, 512], F32, EI)
            self.cbias_d = dr("cbias", [L, 8, 8, 128, 512], F32, EI)
            self.flag_d = dr("flag", [128, 1], F32, EI)
            self.out_d = dr("out", [NT, D], F32, EO)
        if mode == "kv":
            self.kx_own = dr("kx_own", [KROWS, NT], BF16, EO)
            self.vx_own = dr("vx_own", [NT, 3072], BF16, EO)
        elif mode == "main":
            self.kx_own = dr("kx_own", [KROWS, NT], BF16, IN)
            self.vx_own = dr("vx_own", [NT, 3072], BF16, IN)
            self.kx_rem = dr("kx_rem", [KROWS, NT], BF16, EI)
            self.vx_rem = dr("vx_rem", [NT, 3072], BF16, EI)
        else:
            self.kx_own = dr("kx_own", [KROWS, NT], BF16, IN)
            self.vx_own = dr("vx_own", [NT, 3072], BF16, IN)
            self.kchunks = []
            r0 = 0
            while r0 < KROWS:
                n = 64 if r0 == KR_AR else 256
                self.kchunks.append((r0, n, dr("kxa%d" % len(self.kchunks), [2 * n, NT], BF16, IN), Buf()))
                r0 += n
            self.vchunks = [(t0, 256, dr("vxa%d" % (t0 // 256), [512, 3072], BF16, IN), Buf())
                            for t0 in range(0, NT, 256)]
        if mode != "kv":
            self.qn_a = dr("qn_a", [8, 128, NT], BF16, IN)
            self.qr_a = dr("qr_a", [8, 64, NT], BF16, IN)
            self.qt_b = dr("qt_b", [8, 128, NT], BF16, IN)
            self.qt_c = dr("qt_c", [8, 128, NT], BF16, IN)
            self.zs_d = dr("zs", [3, NT, 1024], BF16, IN)
            self.gt_d = dr("gt", [48, 128, NT], BF16, IN)
            self.brt_d = dr("brt", [3, 8, 128, NT], BF16, IN)
            self.x1_d = dr("x1", [NT, D], F32, IN)
        self.b_kx_own, self.b_vx_own, self.b_kx_rem, self.b_vx_rem = Buf(True), Buf(True), Buf(), Buf()
        self.b_q, self.b_zs, self.b_gt, self.b_brt, self.b_x1 = Buf(True), Buf(True), Buf(True), Buf(True), Buf(True)

        with contextlib.ExitStack() as st:
            self.S = S = Sched(nc, st)
            self.ident = self.sb(st, "ident", [128, 128], BF16)
            self.gb = self.sb(st, "gb", [128, GTOT], F32)
            self.ropeA = self.sb(st, "ropeA", [128, NTT, 64], F32)
            self.ropeB = self.sb(st, "ropeB", [128, NTT, 16], F32)
            self.big = self.sb(st, "big", [128, 16, NT], BF16)
            self.b_big = [Buf() for _ in range(NTT)]
            self.b_const, self.b_gb = Buf(), Buf()
            self.pb = [st.enter_context(nc.psum_tensor("pb%d" % i, [128, 512], F32)) for i in range(6)]
            self.tb = [st.enter_context(nc.psum_tensor("tb%d" % i, [128, 1024], BF16)) for i in range(2)]
            self.b_pb = [Buf() for _ in range(6)]
            self.b_tb = [Buf() for _ in range(2)]
            self.tbi = 0

            S.dma("pool", self.ident[:], self.ident_d, writes=[self.b_const])
            S.dma("sp", self.ropeA[:], self.ropeA_d.rearrange("(t p) c -> p t c", p=128), writes=[self.b_const])
            S.dma("sp", self.ropeB[:], self.ropeB_d.rearrange("(t p) c -> p t c", p=128), writes=[self.b_const])
            if mode != "kv":
                self.flag = self.sb(st, "flag", [128, 1], F32)
                self.m4 = self.sb(st, "m4", [128, 4, 512], BF16)
                self.mc = self.sb(st, "mc", [128, 8, 512], BF16)
                self.lam = self.sb(st, "lam", [128, 16], F32)
                self.gsub = self.sb(st, "gsub", [128, 128], F32)
                S.dma("sp", self.flag[:], self.flag_d, writes=[self.b_const])
                S.dma("pool", self.m4[:], self.m4_d.rearrange("j p q -> p j q"), writes=[self.b_const])
                S.dma("pool", self.mc[:], self.mc_d.rearrange("j p q -> p j q"), writes=[self.b_const])

            for l in range(L):
                xsrc = self.x_in if l == 0 else self.x1_d
                S.barrier()
                S.dma("sp", self.gb[:], self.gbd[l], writes=[self.b_gb])
                self.phase_A(l, xsrc)
                S.barrier()
                self.phase_B(l)
                if mode == "fused":
                    self.issue_coll(1000)
                S.barrier()
                if mode == "kv":
                    continue
                self.phase_C(l)
                S.barrier()
                self.phase_D(l)
                S.barrier()
                xdst = self.out_d if l == L - 1 else self.x1_d
                self.phase_E(l, xsrc, xdst)
            S.barrier()
            S.emit()

    def exchange(self):
        rg = [[0, 1], [2, 3], [4, 5], [6, 7]]
        for (r0, n, t, b) in self.kchunks:
            self.coll_q.append((self.kx_own[r0:r0 + n, :], t, self.b_kx_own, b, rg))
        for (t0, n, t, b) in self.vchunks:
            self.coll_q.append((self.vx_own[t0:t0 + n, :], t, self.b_vx_own, b, rg))

    def issue_coll(self, k):
        for _ in range(min(k, len(self.coll_q))):
            src, t, b_src, b, rg = self.coll_q.pop(0)
            self.S.op("pool", (lambda e, src=src, t=t, rg=rg: e.collective_compute(
                "AllGather", ALU.bypass, replica_groups=rg, ins=[src], outs=[t])),
                reads=[b_src], writes=[b], coll=True)

    def k_rem(self, r0, n):
        if self.mode == "main":
            return self.kx_rem[r0:r0 + n, :], self.b_kx_rem
        for (c0, cn, t, b) in self.kchunks:
            if c0 <= r0 and r0 + n <= c0 + cn:
                return t[r0 - c0:r0 - c0 + n, :], b
        raise AssertionError("k_rem rows straddle chunks")

    def v_rem_load(self, v_t, b_v, vcol):
        S, NTT = self.S, self.NTT
        if self.mode == "main":
            S.dma("sp", v_t[:, 0:NTT, 0:128],
                  self.vx_rem[:, vcol:vcol + 128].rearrange("(t p) c -> p t c", p=128),
                  reads=[self.b_vx_rem], writes=[b_v])
            return
        for (t0, n, t, b) in self.vchunks:
            S.dma("sp", v_t[:, t0 // 128:(t0 + n) // 128, 0:128],
                  t[0:n, vcol:vcol + 128].rearrange("(t p) c -> p t c", p=128), reads=[b], writes=[b_v])

    def phase_A(self, l, xsrc):
        S, NTT = self.S, self.NTT
        with contextlib.ExitStack() as st:
            xt = Ring([self.sb(st, "xt", [128, D], F32) for _ in range(2)])
            sq = self.sb(st, "sqA", [128, D], F32)
            b_sq = Buf()
            hb = Ring([self.sb(st, "hb", [128, D], BF16) for _ in range(2)])
            stt = Ring([self.sb(st, "stA", [128, 4], F32) for _ in range(2)])
            for t in range(NTT):
                x_t, b_x = xt.next()
                h_t, b_h = hb.next()
                s_t, b_s = stt.next()
                S.dma("sp", x_t[:], xsrc[t * 128:(t + 1) * 128, :], reads=[self.b_x1], writes=[b_x])
                self.ACT(sq[:], x_t[:], AF.Square, [b_x], [b_sq])
                self.RED(s_t[:, 0:1], sq[:], [b_sq], [b_s])
                self.TS("dve", s_t[:, 1:2], s_t[:, 0:1], 1.0 / D, EPS, ALU.mult, ALU.add, [b_s], [b_s])
                self.ACT(s_t[:, 3:4], s_t[:, 1:2], AF.Sqrt, [b_s], [b_s])
                self.RCP(s_t[:, 2:3], s_t[:, 3:4], [b_s], [b_s])
                self.STT(h_t[:], x_t[:], s_t[:, 2:3], self.gb[:, G_PRE:G_PRE + D], ALU.mult, ALU.mult,
                         [b_x, b_s, self.b_gb], [b_h])
                for half in range(2):
                    tb, b_tb = self.tb[half], self.b_tb[half]
                    for j in range(8):
                        kc = half * 8 + j
                        self.TR(tb[:, j * 128:(j + 1) * 128], h_t[:, kc * 128:(kc + 1) * 128], [b_h], [b_tb])
                    self.CP("act" if half == 0 else "dve",
                            self.big[:, half * 8:(half + 1) * 8, t * 128:(t + 1) * 128],
                            tb[:, :].rearrange("p (j c) -> p j c", j=8), [b_tb], [self.b_big[t]])

    def mk_work(self, st):
        self.w_sq = Ring([self.sb(st, "wsq", [128, 512], F32) for _ in range(2)])
        self.w_xn = Ring([self.sb(st, "wxn", [128, 512], F32) for _ in range(2)])
        self.w_xg = Ring([self.sb(st, "wxg", [128, 512], F32) for _ in range(3)])
        self.w_st = Ring([self.sb(st, "wst", [128, 32], F32) for _ in range(4)])
        self.w_ob = Ring([self.sb(st, "wob", [128, 512], BF16) for _ in range(8)])
        self.w_sg = Ring([self.sb(st, "wsg", [128, 4, 128], BF16) for _ in range(3)])
        self.w_rt = Ring([self.sb(st, "wrt", [128, 4, 64], F32) for _ in range(2)])

    def norm_groups(self, src3, bsrc, gain2, out3, bout):
        ng, gs = src3.shape[1], src3.shape[2]
        sq, b_sq = self.w_sq.next()
        xn, b_xn = self.w_xn.next()
        s_t, b_s = self.w_st.next()
        sq3 = sq[:, 0:ng * gs].rearrange("p (g d) -> p g d", g=ng)
        xn3 = xn[:, 0:ng * gs].rearrange("p (g d) -> p g d", g=ng)
        self.ACT(sq3, src3, AF.Square, [bsrc], [b_sq])
        self.RED(s_t[:, 0:ng], sq3, [b_sq], [b_s])
        self.TS("dve", s_t[:, 8:8 + ng], s_t[:, 0:ng], 1.0 / gs, EPS, ALU.mult, ALU.add, [b_s], [b_s])
        self.ACT(s_t[:, 24:24 + ng], s_t[:, 8:8 + ng], AF.Sqrt, [b_s], [b_s])
        self.RCP(s_t[:, 16:16 + ng], s_t[:, 24:24 + ng], [b_s], [b_s])
        self.TT("dve", xn3, src3, s_t[:, 16:16 + ng].unsqueeze(2).broadcast_to([128, ng, gs]), ALU.mult,
                [bsrc, b_s], [b_xn])
        self.TT("pool", out3, xn3, gain2.unsqueeze(1).broadcast_to([128, ng, gs]), ALU.mult,
                [b_xn, self.b_gb], [bout])

    def rope(self, x3, bx, cos2, sin2, out3, bout):
        ng, half = x3.shape[1], x3.shape[2] // 2
        rt, b_rt = self.w_rt.next()
        c3 = cos2.unsqueeze(1).broadcast_to([128, ng, half])
        s3 = sin2.unsqueeze(1).broadcast_to([128, ng, half])
        x1, x2 = x3[:, :, 0:half], x3[:, :, half:2 * half]
        tv = lambda i: rt[:, i, 0:ng * half].rearrange("p (g d) -> p g d", g=ng)
        rd = [bx, self.b_const]
        self.TT("pool", tv(0), x1, c3, ALU.mult, rd, [b_rt])
        self.TT("pool", tv(1), x2, s3, ALU.mult, rd, [b_rt])
        self.TT("pool", tv(2), x2, c3, ALU.mult, rd, [b_rt])
        self.TT("pool", tv(3), x1, s3, ALU.mult, rd, [b_rt])
        self.TT("pool", out3[:, :, 0:half], tv(0), tv(1), ALU.subtract, [b_rt], [bout])
        self.TT("pool", out3[:, :, half:2 * half], tv(2), tv(3), ALU.add, [b_rt], [bout])

    def transpose_store(self, pieces, bsrc, dst, bdst, q="sp"):
        self.tail_cur.append(lambda: self._transpose_store(pieces, bsrc, dst, bdst, q))

    def _transpose_store(self, pieces, bsrc, dst, bdst, q="sp"):
        n = len(pieces)
        w = pieces[0].shape[1]
        tb, b_tb = self.tb[self.tbi], self.b_tb[self.tbi]
        self.tbi ^= 1
        sg, b_sg = self.w_sg.next()
        for j, p in enumerate(pieces):
            self.TR(tb[0:w, j * 128:(j + 1) * 128], p, [bsrc], [b_tb])
        self.CP("act", sg[0:w, 0:n, :], tb[0:w, 0:n * 128].rearrange("p (j c) -> p j c", j=n), [b_tb], [b_sg])
        self.S.dma(q, dst, sg[0:w, 0:n, :], reads=[b_sg], writes=[bdst])

    def phase_B(self, l):
        S, NT, NTT, NQB, mode = self.S, self.NT, self.NTT, self.NQB, self.mode
        gb = self.gb
        with contextlib.ExitStack() as st:
            self.mk_work(st)
            wbuf = Ring([self.sb(st, "wbuf", [128, 16, 512], BF16) for _ in range(2)])
            cqT = self.sb(st, "cqT", [128, 4, NT], BF16)
            ckvT = self.sb(st, "ckvT", [128, 2, NT], BF16)
            b_cqT = [Buf() for _ in range(NTT)]
            b_ckvT = [Buf() for _ in range(NTT)]
            pbi = [0]

            def next_pb():
                i = pbi[0]
                pbi[0] = (i + 1) % 4
                return self.pb[i], self.b_pb[i]

            blocks = []
            W = self.w_in[l]
            full = mode != "kv"
            blocks.append((W[:, 512:832], 16, "a_ckv", None))
            for i in range(4):
                blocks.append((self.w_ukv[l][:, i * 512:(i + 1) * 512], 2, "ukv", i))
            for i in range(2):
                blocks.append((W[:, 2880 + i * 512:2880 + (i + 1) * 512], 16, "bqk", ("k", i)))
            for i in range(2):
                blocks.append((W[:, 3904 + i * 512:3904 + (i + 1) * 512], 16, "v", 1024 + i * 512))
            for i in range(2):
                blocks.append((W[:, 6976 + i * 512:6976 + (i + 1) * 512], 16, "cqk", ("k", i)))
            for i in range(2):
                blocks.append((W[:, 8000 + i * 512:8000 + (i + 1) * 512], 16, "v", 2048 + i * 512))
            n_kv_blocks = len(blocks)
            if full:
                blocks.append((W[:, 0:512], 16, "a_cq", None))
                for i in range(2):
                    blocks.append((W[:, 832 + i * 512:832 + (i + 1) * 512], 16, "z", (0, i * 512)))
                for i in range(2):
                    blocks.append((W[:, 1856 + i * 512:1856 + (i + 1) * 512], 16, "bqk", ("q", i)))
                for i in range(2):
                    blocks.append((W[:, 4928 + i * 512:4928 + (i + 1) * 512], 16, "z", (1, i * 512)))
                for i in range(2):
                    blocks.append((W[:, 5952 + i * 512:5952 + (i + 1) * 512], 16, "cqk", ("q", i)))
                for i in range(2):
                    blocks.append((W[:, 9024 + i * 512:9024 + (i + 1) * 512], 16, "z", (2, i * 512)))
                for i in range(12):
                    blocks.append((W[:, 10048 + i * 512:10048 + (i + 1) * 512], 16, "gate", i))
                for i in range(4):
                    blocks.append((self.w_uq[l][:, i * 384:(i + 1) * 384], 4, "uq", i))

            def load(bi):
                wsrc, nkc, kind, args = blocks[bi]
                w_t, b_w = wbuf.next()
                ncols = wsrc.shape[1]
                S.dma("pool", w_t[:, 0:nkc, 0:ncols], wsrc.rearrange("(kc p) n -> p kc n", p=128), writes=[b_w])
                return w_t, b_w

            nxt = load(0)
            tails = []
            self.coll_q = []
            for bi, (wsrc, nkc, kind, args) in enumerate(blocks):
                w_t, b_w = nxt
                if bi + 1 < len(blocks):
                    nxt = load(bi + 1)
                self.issue_coll(2)
                ncols = wsrc.shape[1]
                if kind == "gate":
                    for c4 in range(4):
                        g = args * 4 + c4
                        for tbk in range(NQB):
                            ps, b_ps = next_pb()
                            for kc in range(16):
                                self.MM(ps[:, :], w_t[:, kc, c4 * 128:(c4 + 1) * 128],
                                        self.big[:, kc, tbk * 512:(tbk + 1) * 512], kc == 0, kc == 15,
                                        [b_w] + self.b_big[tbk * 4:(tbk + 1) * 4], [b_ps])
                            ob, b_ob = self.w_ob.next()
                            self.ACT(ob[:, :], ps[:, :], AF.Sigmoid, [b_ps], [b_ob])
                            S.dma("sp", self.gt_d[g, :, tbk * 512:(tbk + 1) * 512], ob[:, :], reads=[b_ob],
                                  writes=[self.b_gt])
                    continue
                for t in range(NTT):
                    ps, b_ps = next_pb()
                    tsl = slice(t * 128, (t + 1) * 128)
                    if kind == "uq":
                        lhs = lambda kc: cqT[:, kc, tsl]
                        b_l = b_cqT[t]
                    elif kind == "ukv":
                        lhs = lambda kc: ckvT[:, kc, tsl]
                        b_l = b_ckvT[t]
                    else:
                        lhs = lambda kc: self.big[:, kc, tsl]
                        b_l = self.b_big[t]
                    for kc in range(nkc):
                        self.MM(ps[:, 0:ncols], lhs(kc), w_t[:, kc, 0:ncols], kc == 0, kc == nkc - 1,
                                [b_w, b_l], [b_ps])
                    self.tail_cur = []
                    self.post_B(kind, args, t, ps, b_ps, cqT, b_cqT, ckvT, b_ckvT)
                    tails.append(self.tail_cur)
                    if len(tails) > 2:
                        for f in tails.pop(0):
                            f()
                nxt_kind = blocks[bi + 1][2] if bi + 1 < len(blocks) else None
                if nxt_kind is None or bi == n_kv_blocks - 1 or (nxt_kind in ("uq", "ukv") and nxt_kind != kind):
                    while tails:
                        for f in tails.pop(0):
                            f()
                if bi == n_kv_blocks - 1 and mode == "fused":
                    self.exchange()

    def post_B(self, kind, args, t, ps, b_ps, cqT, b_cqT, ckvT, b_ckvT):
        S, NT, gb = self.S, self.NT, self.gb
        tsl = slice(t * 128, (t + 1) * 128)
        if kind == "z":
            n, off = args
            ob, b_ob = self.w_ob.next()
            self.ACT(ob[:, :], ps[:, :], AF.Silu, [b_ps], [b_ob])
            S.dma("sp", self.zs_d[n, tsl, off:off + 512], ob[:, :], reads=[b_ob], writes=[self.b_zs])
        elif kind == "v":
            ob, b_ob = self.w_ob.next()
            self.CP("dve", ob[:, :], ps[:, :], [b_ps], [b_ob])
            S.dma("sp", self.vx_own[tsl, args:args + 512], ob[:, :], reads=[b_ob], writes=[self.b_vx_own])
        elif kind == "a_cq":
            ob, b_ob = self.w_ob.next()
            self.norm_groups(ps[:, :].rearrange("p (g d) -> p g d", g=1), b_ps, gb[:, G_CQ:G_CQ + 512],
                             ob[:, :].rearrange("p (g d) -> p g d", g=1), b_ob)
            def tail(ob=ob, b_ob=b_ob):
                tb, b_tb = self.tb[self.tbi], self.b_tb[self.tbi]
                self.tbi ^= 1
                for j in range(4):
                    self.TR(tb[:, j * 128:(j + 1) * 128], ob[:, j * 128:(j + 1) * 128], [b_ob], [b_tb])
                self.CP("act", cqT[:, :, tsl], tb[:, 0:512].rearrange("p (j c) -> p j c", j=4), [b_tb], [b_cqT[t]])
            self.tail_cur.append(tail)
        elif kind == "a_ckv":
            ob, b_ob = self.w_ob.next()
            self.norm_groups(ps[:, 0:256].rearrange("p (g d) -> p g d", g=1), b_ps, gb[:, G_CKV:G_CKV + 256],
                             ob[:, 0:256].rearrange("p (g d) -> p g d", g=1), b_ob)
            def tail(ob=ob, b_ob=b_ob):
                tb, b_tb = self.tb[self.tbi], self.b_tb[self.tbi]
                self.tbi ^= 1
                for j in range(2):
                    self.TR(tb[:, j * 128:(j + 1) * 128], ob[:, j * 128:(j + 1) * 128], [b_ob], [b_tb])
                self.CP("act", ckvT[:, :, tsl], tb[:, 0:256].rearrange("p (j c) -> p j c", j=2), [b_tb], [b_ckvT[t]])
            self.tail_cur.append(tail)
            xg, b_xg = self.w_xg.next()
            self.norm_groups(ps[:, 256:320].rearrange("p (g d) -> p g d", g=1), b_ps,
                             gb[:, G_AK + 128:G_AK + 192], xg[:, 0:64].rearrange("p (g d) -> p g d", g=1), b_xg)
            ob2, b_ob2 = self.w_ob.next()
            self.rope(xg[:, 0:64].rearrange("p (g d) -> p g d", g=1), b_xg, self.ropeA[:, t, 0:32],
                      self.ropeA[:, t, 32:64], ob2[:, 0:64].rearrange("p (g d) -> p g d", g=1), b_ob2)
            self.transpose_store([ob2[:, 0:64]], b_ob2,
                                 self.kx_own[KR_AR:KR_AR + 64, tsl].rearrange("(j p) t -> p j t", j=1),
                                 self.b_kx_own)
        elif kind == "bqk":
            which, i = args
            goff = G_BQ if which == "q" else G_BK
            xg, b_xg = self.w_xg.next()
            xg3 = xg[:, :].rearrange("p (g d) -> p g d", g=8)
            self.norm_groups(ps[:, :].rearrange("p (g d) -> p g d", g=8), b_ps, gb[:, goff:goff + 64], xg3, b_xg)
            ob, b_ob = self.w_ob.next()
            ob3 = ob[:, :].rearrange("p (g d) -> p g d", g=8)
            self.CP("pool", ob[:, :], xg[:, :], [b_xg], [b_ob])
            self.rope(xg3[:, :, 0:16], b_xg, self.ropeB[:, t, 0:8], self.ropeB[:, t, 8:16], ob3[:, :, 0:16], b_ob)
            if which == "q":
                dst = self.qt_b[i * 4:(i + 1) * 4, :, tsl].rearrange("h p t -> p h t")
                bd = self.b_q
            else:
                dst = self.kx_own[KR_B + i * 512:KR_B + (i + 1) * 512, tsl].rearrange("(h p) t -> p h t", p=128)
                bd = self.b_kx_own
            self.transpose_store([ob[:, j * 128:(j + 1) * 128] for j in range(4)], b_ob, dst, bd)
        elif kind == "cqk":
            which, i = args
            goff = G_CQ2 if which == "q" else G_CK
            ob, b_ob = self.w_ob.next()
            self.norm_groups(ps[:, :].rearrange("p (g d) -> p g d", g=4), b_ps, gb[:, goff:goff + 128],
                             ob[:, :].rearrange("p (g d) -> p g d", g=4), b_ob)
            if which == "q":
                dst = self.qt_c[i * 4:(i + 1) * 4, :, tsl].rearrange("h p t -> p h t")
                bd = self.b_q
            else:
                dst = self.kx_own[KR_C + i * 512:KR_C + (i + 1) * 512, tsl].rearrange("(h p) t -> p h t", p=128)
                bd = self.b_kx_own
            self.transpose_store([ob[:, j * 128:(j + 1) * 128] for j in range(4)], b_ob, dst, bd)
        elif kind == "uq":
            i = args
            ps3 = ps[:, 0:384].rearrange("p (h d) -> p h d", h=2)
            ob, b_ob = self.w_ob.next()
            ob3 = ob[:, 0:384].rearrange("p (h d) -> p h d", h=2)
            self.norm_groups(ps3[:, :, 0:128], b_ps, gb[:, G_AQ:G_AQ + 128], ob3[:, :, 0:128], b_ob)
            xg, b_xg = self.w_xg.next()
            xg3 = xg[:, 0:128].rearrange("p (h d) -> p h d", h=2)
            self.norm_groups(ps3[:, :, 128:192], b_ps, gb[:, G_AQ + 128:G_AQ + 192], xg3, b_xg)
            self.rope(xg3, b_xg, self.ropeA[:, t, 0:32], self.ropeA[:, t, 32:64], ob3[:, :, 128:192], b_ob)
            self.transpose_store([ob3[:, j, 0:128] for j in range(2)], b_ob,
                                 self.qn_a[2 * i:2 * i + 2, :, tsl].rearrange("h p t -> p h t"), self.b_q)
            self.transpose_store([ob3[:, j, 128:192] for j in range(2)], b_ob,
                                 self.qr_a[2 * i:2 * i + 2, :, tsl].rearrange("h p t -> p h t"), self.b_q)
        elif kind == "ukv":
            i = args
            ps3 = ps[:, :].rearrange("p (h d) -> p h d", h=2)
            ob, b_ob = self.w_ob.next()
            ob3 = ob[:, :].rearrange("p (h d) -> p h d", h=2)
            self.norm_groups(ps3[:, :, 0:128], b_ps, gb[:, G_AK:G_AK + 128], ob3[:, :, 0:128], b_ob)
            self.CP("dve", ob3[:, :, 128:256], ps3[:, :, 128:256], [b_ps], [b_ob])
            self.transpose_store([ob3[:, j, 0:128] for j in range(2)], b_ob,
                                 self.kx_own[KR_A + i * 256:KR_A + (i + 1) * 256, tsl].rearrange(
                                     "(h p) t -> p h t", p=128), self.b_kx_own)
            S.dma("sp", self.vx_own[tsl, i * 256:(i + 1) * 256].rearrange("t (h d) -> t h d", h=2),
                  ob3[:, :, 128:256], reads=[b_ob], writes=[self.b_vx_own])

    def phase_C(self, l):
        S, NT, NTT, NQB, gb = self.S, self.NT, self.NTT, self.NQB, self.gb
        with contextlib.ExitStack() as st:
            self.w_sq = Ring([self.sb(st, "wsq", [128, 128], F32) for _ in range(2)])
            self.w_st = Ring([self.sb(st, "wst", [128, 32], F32) for _ in range(8)])
            kT = Ring([self.sb(st, "kT", [128, 2 * NT], BF16) for _ in range(2)])
            krT = self.sb(st, "krT", [128, 2 * NT], BF16)
            b_krT = Buf()
            self.MEMSET("pool", krT[64:128, :], 0.0, [b_krT])
            qT = Ring([self.sb(st, "qT", [128, NT], BF16) for _ in range(2)])
            qrT = Ring([self.sb(st, "qrT", [128, NT], BF16) for _ in range(2)])
            for q_, b_ in zip(qrT.t, qrT.b):
                self.MEMSET("pool", q_[64:128, :], 0.0, [b_])
            vv = Ring([self.sb(st, "vv", [128, 2 * NTT, 132], BF16) for _ in range(2)])
            zz = Ring([self.sb(st, "zz", [128, NTT, 128], BF16) for _ in range(2)])
            pt = Ring([self.sb(st, "pt", [128, 512], BF16) for _ in range(4)])
            eb = Ring([self.sb(st, "eb", [128, 8, 512], BF16) for _ in range(2)])
            bias = Ring([self.sb(st, "bias", [128, 512], F32) for _ in range(3)])
            ebf = Ring([self.sb(st, "ebf", [128, 512], BF16) for _ in range(2)])
            of = Ring([self.sb(st, "of", [128, 128], F32) for _ in range(6)])
            o0 = Ring([self.sb(st, "o0", [128, 4, 128], F32) for _ in range(2)])
            ob = Ring([self.sb(st, "obC", [128, 128], BF16) for _ in range(9)])
            stg = Ring([self.sb(st, "stg", [128, 512], BF16) for _ in range(3)])
            for v_t, b_v in zip(vv.t, vv.b):
                self.MEMSET("pool", v_t[:, :, 128:129], 1.0, [b_v])
            lam, b_lam = self.lam, Buf()
            sq, b_sq = self.w_sq.next()
            lv = gb[:, G_LAM:G_LAM + 256].rearrange("p (a d) -> p a d", a=4)
            self.TT("dve", sq[:, 0:64], lv[:, 0, :], lv[:, 1, :], ALU.mult, [self.b_gb], [b_sq])
            self.TT("dve", sq[:, 64:128], lv[:, 2, :], lv[:, 3, :], ALU.mult, [self.b_gb], [b_sq])
            self.RED(lam[:, 0:2], sq[:, 0:128].rearrange("p (a d) -> p a d", a=2), [b_sq], [b_lam])
            self.ACT(lam[:, 2:4], lam[:, 0:2], AF.Exp, [b_lam], [b_lam])
            self.TT("dve", lam[:, 4:5], lam[:, 2:3], lam[:, 3:4], ALU.subtract, [b_lam], [b_lam])
            self.TT("dve", lam[:, 5:6], lam[:, 4:5], gb[:, G_LAMC:G_LAMC + 1], ALU.add, [b_lam, self.b_gb], [b_lam])
            self.TS("dve", lam[:, 6:7], lam[:, 5:6], -1.0, None, ALU.mult, None, [b_lam], [b_lam])
            self.TS("dve", self.gsub[:, :], gb[:, G_BSUB:G_BSUB + 128], gb[:, G_LAMC + 1:G_LAMC + 2], None,
                    ALU.mult, None, [self.b_gb], [b_lam])
            neglam = lam[:, 6:7]

            sbank = [(self.pb[i][:, :], self.b_pb[i]) for i in (0, 1, 4, 5)]
            sbank.append((self.tb[1][:, :].bitcast(F32), self.b_tb[1]))
            pend = []
            LOOK = 4
            self.c_tails = []
            o0_cur = [None]
            obank = [[(self.pb[2 + i], self.b_pb[2 + i]) for i in range(2)]]
            cnt = {"s": 0, "o": 0}

            for br in ("A", "B", "C"):
                n = "ABC".index(br)
                scale = {"A": 192.0 ** -0.5, "B": 64.0 ** -0.5, "C": 128.0 ** -0.5}[br]
                krow = {"A": KR_A, "B": KR_B, "C": KR_C}[br]
                qsrc = {"A": self.qn_a, "B": self.qt_b, "C": self.qt_c}[br]
                if br == "A":
                    ap_, b_ = self.k_rem(KR_AR, 64)
                    S.dma("sp", krT[0:64, 0:NT], ap_, reads=[b_], writes=[b_krT])
                    S.dma("sp", krT[0:64, NT:2 * NT], self.kx_own[KR_AR:KR_AR + 64, :], reads=[self.b_kx_own],
                          writes=[b_krT])
                if br == "B":
                    for q_, b_ in zip(qT.t, qT.b):
                        self.MEMSET("pool", q_[64:128, :], 0.0, [b_])
                    for q_, b_ in zip(qrT.t, qrT.b):
                        self.MEMSET("pool", q_[0:64, :], 0.0, [b_])
                for h in range(8):
                    k_t, b_k = kT.next()
                    q_t, b_qt = qT.next()
                    v_t, b_v = vv.next()
                    z_t, b_z = zz.next()
                    ap_, b_ = self.k_rem(krow + h * 128, 128)
                    S.dma("sp", k_t[:, 0:NT], ap_, reads=[b_], writes=[b_k])
                    S.dma("sp", k_t[:, NT:2 * NT], self.kx_own[krow + h * 128:krow + (h + 1) * 128, :],
                          reads=[self.b_kx_own], writes=[b_k])
                    if br == "B":
                        qr_t, b_qr = qrT.next()
                        S.dma("sp", q_t[0:64, :], qsrc[h][0:64, :], reads=[self.b_q], writes=[b_qt])
                        S.dma("sp", qr_t[64:128, :], qsrc[h][64:128, :], reads=[self.b_q], writes=[b_qr])
                    else:
                        S.dma("sp", q_t[:, :], qsrc[h], reads=[self.b_q], writes=[b_qt])
                    if br == "A":
                        qr_t, b_qr = qrT.next()
                        S.dma("sp", qr_t[0:64, :], self.qr_a[h], reads=[self.b_q], writes=[b_qr])
                    vcol = n * 1024 + h * 128
                    self.v_rem_load(v_t, b_v, vcol)
                    S.dma("sp", v_t[:, NTT:2 * NTT, 0:128],
                          self.vx_own[:, vcol:vcol + 128].rearrange("(t p) c -> p t c", p=128),
                          reads=[self.b_vx_own], writes=[b_v])
                    self.TS("dve", v_t[:, 0:NTT, 0:129], v_t[:, 0:NTT, 0:129], self.flag[:, 0:1], None, ALU.mult,
                            None, [b_v, self.b_const], [b_v])
                    S.dma("sp", z_t[:, :, :],
                          self.zs_d[n, :, h * 128:(h + 1) * 128].rearrange("(t p) c -> p t c", p=128),
                          reads=[self.b_zs], writes=[b_z])
                    if br == "C":
                        def prep_eb(hh):
                            e_t, b_e = eb.next()
                            for jj in range(8):
                                bi_t, b_bi = bias.next()
                                ef_t, b_ef = ebf.next()
                                S.dma("sp", bi_t[:, :], self.cbias_d[l, hh, jj], writes=[b_bi])
                                self.ACT(ef_t[:, :], bi_t[:, :], AF.Exp, [b_bi], [b_ef])
                                self.TT("pool", e_t[:, jj, :], ef_t[:, :], self.mc[:, jj, :], ALU.mult,
                                        [b_ef, self.b_const], [b_e])
                            return e_t, b_e
                        if h == 0:
                            eb_next = prep_eb(0)
                        e_t, b_e = eb_next
                    for qb in range(NQB):
                        if br == "C" and qb == min(1, NQB - 1) and h + 1 < 8:
                            eb_next = prep_eb(h + 1)
                        qsl = slice(qb * 512, (qb + 1) * 512)
                        if br == "C":
                            tiles = []
                            for j in range(-4, 4):
                                ti = 4 * qb + j
                                tiles.append((NTT + ti if ti >= 0 else NTT + ti, j))
                        else:
                            tiles = [(i, None) for i in range(NTT)]
                            tiles += [(NTT + i, None) for i in range(4 * qb)]
                            tiles += [(NTT + 4 * qb + j, j) for j in range(4)]
                        for m in range(2 if br == "B" else 1):
                            oset = obank[0]
                            cnt["o"] += 1
                            grp = {"started": [False] * 2}
                            o0_t = b_o0 = None
                            if br == "B" and m == 0:
                                o0_t, b_o0 = o0.next()
                                o0_cur[0] = (o0_t, b_o0)
                            elif br == "B":
                                o0_t, b_o0 = o0_cur[0]
                            ctx = dict(br=br, n=n, h=h, m=m, qb=qb, qsl=qsl, scale=scale, k_t=k_t, b_k=b_k, q_t=q_t,
                                       b_qt=b_qt, v_t=v_t, b_v=b_v, z_t=z_t, b_z=b_z, oset=oset, grp=grp,
                                       o0_t=o0_t, b_o0=b_o0)
                            if br in ("A", "B"):
                                ctx.update(qr_t=qr_t, b_qr=b_qr)
                            if br == "C":
                                ctx.update(e_t=e_t, b_e=b_e)
                            for idx, (ti, j) in enumerate(tiles):
                                job = dict(ctx, ti=ti, j=j, final=(idx == len(tiles) - 1))
                                self.c_qk(job, sbank, cnt, krT, b_krT)
                                pend.append(job)
                                if len(pend) > LOOK:
                                    self.c_step(pend.pop(0), pt, of, ob, stg, neglam, b_lam)
            while pend:
                self.c_step(pend.pop(0), pt, of, ob, stg, neglam, b_lam)
            for _ in range(3):
                self.c_step(None, pt, of, ob, stg, neglam, b_lam)

    def c_step(self, J, pt, of, ob, stg, neglam, b_lam):
        self.c_tails.append(self.c_rest(J, pt, of, ob, stg, neglam, b_lam) if J is not None else None)
        if len(self.c_tails) > 2:
            f = self.c_tails.pop(0)
            if f is not None:
                f()

    def c_qk(self, J, sbank, cnt, krT, b_krT):
        br, m = J["br"], J["m"]
        ti, qsl = J["ti"], J["qsl"]
        ksl = slice(ti * 128, (ti + 1) * 128)
        sp, b_sp = sbank[cnt["s"] % len(sbank)]
        cnt["s"] += 1
        J["sp"], J["b_sp"] = sp, b_sp
        k_t, q_t, b_k, b_qt = J["k_t"], J["q_t"], J["b_k"], J["b_qt"]
        if br == "A":
            self.MM(sp, k_t[:, ksl], q_t[:, qsl], True, False, [b_k, b_qt], [b_sp])
            self.MM(sp, krT[:, ksl], J["qr_t"][:, qsl], False, True, [b_krT, J["b_qr"]], [b_sp])
        elif br == "B":
            if m == 0:
                self.MM(sp, k_t[:, ksl], q_t[:, qsl], True, True, [b_k, b_qt], [b_sp])
            else:
                self.MM(sp, k_t[:, ksl], J["qr_t"][:, qsl], True, True, [b_k, J["b_qr"]], [b_sp])
        else:
            self.MM(sp, k_t[:, ksl], q_t[:, qsl], True, True, [b_k, b_qt], [b_sp])

    def c_rest(self, J, pt, of, ob, stg, neglam, b_lam):
        S = self.S
        br, m, j, ti, qb = J["br"], J["m"], J["j"], J["ti"], J["qb"]
        sp, b_sp, v_t, b_v, oset, grp = J["sp"], J["b_sp"], J["v_t"], J["b_v"], J["oset"], J["grp"]
        p_t, b_p = pt.next()
        self.ACT(p_t[:, :], sp, AF.Exp, [b_sp], [b_p], scale=J["scale"])
        if br == "C":
            self.TT("dve", p_t[:, :], p_t[:, :], J["e_t"][:, j + 4, :], ALU.mult, [b_p, J["b_e"]], [b_p])
            jqs = [jq for jq in range(4) if (jq <= j + 4 if j < 0 else jq >= j)]
        elif j is not None:
            self.TT("dve", p_t[:, :], p_t[:, :], self.m4[:, j, :], ALU.mult, [b_p, self.b_const], [b_p])
            jqs = [jq for jq in range(4) if jq >= j]
        else:
            jqs = [0, 1, 2, 3]
        for jq in jqs:
            o_t, b_o = oset[jq // 2]
            last = (j is not None and j == jq)
            self.MM(o_t[:, (jq % 2) * 256:(jq % 2) * 256 + 129], p_t[:, jq * 128:(jq + 1) * 128],
                    v_t[:, ti, 0:129], not grp["started"][jq // 2], last, [b_p, b_v], [b_o], sgc=True)
            grp["started"][jq // 2] = True
        if not J["final"]:
            return None
        z_t, b_z, n, h, qsl = J["z_t"], J["b_z"], J["n"], J["h"], J["qsl"]
        o0_t, b_o0 = J["o0_t"], J["b_o0"]
        first_map = (br == "B" and m == 0)
        obs = []
        if br == "B" and not first_map:
            s2, b_s2 = self.w_st.next()
            ofs = []
            for jq in range(4):
                o_t, b_o = oset[jq // 2]
                c0 = (jq % 2) * 256
                s_t, b_s = self.w_st.next()
                self.RCP(s_t[:, 0:1], o_t[:, c0 + 128:c0 + 129], [b_o], [b_s])
                of_t, b_of = of.next()
                self.TS("dve", of_t[:, :], o_t[:, c0:c0 + 128], s_t[:, 0:1], None, ALU.mult, None, [b_o, b_s], [b_of])
                self.STT(of_t[:, :], of_t[:, :], neglam, o0_t[:, jq, :], ALU.mult, ALU.add,
                         [b_of, b_lam, b_o0], [b_of])
                sq, b_sq = self.w_sq.next()
                self.ACT(sq[:, 0:128], of_t[:, :], AF.Square, [b_of], [b_sq])
                self.RED(s2[:, jq:jq + 1], sq[:, 0:128], [b_sq], [b_s2])
                ofs.append((of_t, b_of))
            self.TS("dve", s2[:, 4:8], s2[:, 0:4], 1.0 / 128, EPS, ALU.mult, ALU.add, [b_s2], [b_s2])
            self.ACT(s2[:, 8:12], s2[:, 4:8], AF.Sqrt, [b_s2], [b_s2])
            self.RCP(s2[:, 12:16], s2[:, 8:12], [b_s2], [b_s2])
            for jq in range(4):
                of_t, b_of = ofs[jq]
                tok = qb * 4 + jq
                ob_t, b_ob = ob.next()
                self.STT(of_t[:, :], of_t[:, :], s2[:, 12 + jq:13 + jq], self.gsub[:, :], ALU.mult, ALU.mult,
                         [b_of, b_s2, b_lam], [b_of])
                self.TT("pool", ob_t[:, :], of_t[:, :], z_t[:, tok, :], ALU.mult, [b_of, b_z], [b_ob])
                obs.append((ob_t, b_ob))
        for jq in (range(4) if not (br == "B" and not first_map) else ()):
            o_t, b_o = oset[jq // 2]
            c0 = (jq % 2) * 256
            ov = o_t[:, c0:c0 + 128]
            s_t, b_s = self.w_st.next()
            tok = qb * 4 + jq
            self.RCP(s_t[:, 0:1], o_t[:, c0 + 128:c0 + 129], [b_o], [b_s])
            if first_map:
                self.TS("dve", o0_t[:, jq, :], ov, s_t[:, 0:1], None, ALU.mult, None, [b_o, b_s], [b_o0])
                continue
            ob_t, b_ob = ob.next()
            self.STT(ob_t[:, :], ov, s_t[:, 0:1], z_t[:, tok, :], ALU.mult, ALU.mult, [b_o, b_s, b_z], [b_ob])
            obs.append((ob_t, b_ob))
        if first_map:
            return None

        def tail():
            sg_t, b_sg = stg.next()
            tb, b_tb = self.tb[0], self.b_tb[0]
            for jq, (ob_t, b_ob) in enumerate(obs):
                self.TR(tb[:, jq * 128:(jq + 1) * 128], ob_t[:, :], [b_ob], [b_tb])
            self.CP("act", sg_t[:, :], tb[:, 0:512], [b_tb], [b_sg])
            S.dma("act", self.brt_d[n, h, :, qsl], sg_t[:, :], reads=[b_sg], writes=[self.b_brt])
        return tail

    def phase_D(self, l):
        S, NT, NQB = self.S, self.NT, self.NQB
        TG = min(NT, 1024)
        NG = NT // TG
        NB = TG // 512
        with contextlib.ExitStack() as st:
            brT = self.sb(st, "brT", [128, 24, TG], BF16)
            b_brT = Buf()
            wb = Ring([self.sb(st, "wbD", [128, 24, 256], BF16) for _ in range(2)])
            gtt = Ring([self.sb(st, "gtt", [128, 512], BF16) for _ in range(4)])
            acc = Ring([self.sb(st, "acc", [128, 512], F32) for _ in range(2)])
            tmp = Ring([self.sb(st, "tmpD", [128, 512], F32) for _ in range(2)])
            k = 0
            for g in range(NG):
                S.dma("sp", brT[:, :, :],
                      self.brt_d[:, :, :, g * TG:(g + 1) * TG].rearrange("n h p t -> p (n h) t"),
                      reads=[self.b_brt], writes=[b_brT])
                for cb in range(16):
                    if cb % 2 == 0:
                        w_t, b_w = wb.next()
                        S.dma("pool", w_t[:, :, :],
                              self.w_br[l][:, :, cb * 128:(cb + 2) * 128].rearrange("n (kc p) c -> p (n kc) c", p=128),
                              writes=[b_w])
                    wc = (cb % 2) * 128
                    for tb_ in range(NB):
                        tok0 = g * TG + tb_ * 512
                        a_t, b_a = acc.next()
                        for n in range(3):
                            ps, b_ps = self.pb[k % 4], self.b_pb[k % 4]
                            k += 1
                            for kc in range(8):
                                self.MM(ps[:, :], w_t[:, n * 8 + kc, wc:wc + 128], brT[:, n * 8 + kc, tb_ * 512:(tb_ + 1) * 512],
                                        kc == 0, kc == 7, [b_w, b_brT], [b_ps])
                            g_t, b_g = gtt.next()
                            S.dma("sp", g_t[:, :], self.gt_d[n * 16 + cb, :, tok0:tok0 + 512], reads=[self.b_gt],
                                  writes=[b_g])
                            if n == 0:
                                self.TT("dve", a_t[:, :], ps[:, :], g_t[:, :], ALU.mult, [b_ps, b_g], [b_a])
                            else:
                                t_t, b_t = tmp.next()
                                self.TT("dve", t_t[:, :], ps[:, :], g_t[:, :], ALU.mult, [b_ps, b_g], [b_t])
                                if n == 1:
                                    self.TT("pool", a_t[:, :], a_t[:, :], t_t[:, :], ALU.add, [b_a, b_t], [b_a])
                                else:
                                    self.TT("pool", self.big[:, cb, tok0:tok0 + 512], a_t[:, :], t_t[:, :], ALU.add,
                                            [b_a, b_t], self.b_big[tok0 // 128:tok0 // 128 + 4])

    def phase_E(self, l, xsrc, xdst):
        S, NT, NTT = self.S, self.NT, self.NTT
        with contextlib.ExitStack() as st:
            wo = Ring([self.sb(st, "wo", [128, 16, 512], BF16) for _ in range(2)])
            xc = Ring([self.sb(st, "xc", [128, 512], F32) for _ in range(3)])
            oc = Ring([self.sb(st, "oc", [128, 512], F32) for _ in range(3)])
            b_x1w = Buf(True)
            k = 0
            for obk in range(4):
                w_t, b_w = wo.next()
                csl = slice(obk * 512, (obk + 1) * 512)
                S.dma("pool", w_t[:, :, :], self.w_out[l][:, csl].rearrange("(kc p) n -> p kc n", p=128), writes=[b_w])
                for t in range(NTT):
                    tsl = slice(t * 128, (t + 1) * 128)
                    ps, b_ps = self.pb[k % 4], self.b_pb[k % 4]
                    k += 1
                    for kc in range(16):
                        self.MM(ps[:, :], self.big[:, kc, tsl], w_t[:, kc, :], kc == 0, kc == 15,
                                [b_w, self.b_big[t]], [b_ps])
                    x_t, b_x = xc.next()
                    o_t, b_o = oc.next()
                    S.dma("sp", x_t[:, :], xsrc[tsl, csl], reads=[self.b_x1], writes=[b_x])
                    self.TT("dve", o_t[:, :], ps[:, :], x_t[:, :], ALU.add, [b_ps, b_x], [b_o])
                    S.dma("act", xdst[tsl, csl], o_t[:, :], reads=[b_o], writes=[b_x1w])
            if xdst is self.x1_d:
                self.b_x1.w = b_x1w.w
                self.b_x1.r = []


def _rope_tables(pos, dim, theta):
    inv = (1.0 / (np.float32(theta) ** (np.arange(0, dim, 2, dtype=np.float32) / np.float32(dim)))).astype(np.float32)
    ang = pos.astype(np.float32)[:, None] * inv[None, :]
    return np.concatenate([np.cos(ang), np.sin(ang)], axis=1).astype(np.float32)


def _masks():
    kp = np.arange(128)[:, None]
    qq = np.arange(512)[None, :]
    m4 = np.zeros((4, 128, 512), np.float32)
    for j in range(4):
        m4[j] = ((j * 128 + kp) // 64 <= qq // 64)
    mc = np.zeros((8, 128, 512), np.float32)
    for jj in range(8):
        j = jj - 4
        ck = (j * 128 + kp) // 64
        cq = qq // 64
        mc[jj] = (ck <= cq) & (ck >= cq - 8)
    return m4, mc


def _bias_tiles(c_rel_bias):
    kp = np.arange(128)[:, None]
    qq = np.arange(512)[None, :]
    idx = np.stack([np.clip(qq - kp - (jj - 4) * 128, -63, 128) + 63 for jj in range(8)], 0)
    return np.ascontiguousarray(c_rel_bias[:, :, idx]).astype(np.float32)


def _gain_table(p, l):
    lam_init = 0.8 - 0.6 * math.exp(-0.3 * l)
    row = np.zeros((GTOT,), np.float32)
    row[G_PRE:G_PRE + 2048] = p["g_pre"][l]
    row[G_CQ:G_CQ + 512] = p["a_g_cq"][l]
    row[G_CKV:G_CKV + 256] = p["a_g_ckv"][l]
    row[G_AQ:G_AQ + 192] = p["a_g_q"][l]
    row[G_AK:G_AK + 192] = p["a_g_k"][l]
    row[G_BQ:G_BQ + 64] = p["b_g_q"][l]
    row[G_BK:G_BK + 64] = p["b_g_k"][l]
    row[G_BSUB:G_BSUB + 128] = p["b_g_sub"][l]
    row[G_CQ2:G_CQ2 + 128] = p["c_g_q"][l]
    row[G_CK:G_CK + 128] = p["c_g_k"][l]
    row[G_LAM:G_LAM + 256] = p["b_lam"][l].reshape(-1)
    row[G_LAMC] = lam_init
    row[G_LAMC + 1] = 1.0 - lam_init
    return np.ascontiguousarray(np.broadcast_to(row[None, :], (128, GTOT)))


_PROGS = {}


def _prog(mode, L, NT):
    key = (mode, L, NT)
    if key not in _PROGS:
        _PROGS[key] = Prog(mode, L, NT)
    return _PROGS[key]


def kernel(x, g_pre, w_in, a_g_cq, a_g_ckv, a_w_uq, a_w_ukv, a_g_q, a_g_k, b_g_q, b_g_k,
           b_lam, b_g_sub, c_g_q, c_g_k, c_rel_bias, w_branch, w_out):
    p = dict(g_pre=g_pre, a_g_cq=a_g_cq, a_g_ckv=a_g_ckv, a_g_q=a_g_q, a_g_k=a_g_k, b_g_q=b_g_q, b_g_k=b_g_k,
             b_lam=b_lam, b_g_sub=b_g_sub, c_g_q=c_g_q, c_g_k=c_g_k)
    p = {k: np.asarray(v, np.float32) for k, v in p.items()}
    x = np.asarray(x, np.float32)
    B, S_, _ = x.shape
    L = w_in.shape[0]
    NT = S_ // 2
    ncores = 2 * B
    w_in = np.asarray(w_in, np.float32)
    a_w_uq = np.asarray(a_w_uq, np.float32)
    a_w_ukv = np.asarray(a_w_ukv, np.float32)
    w_branch = np.asarray(w_branch, np.float32)
    w_out = np.asarray(w_out, np.float32)
    gbs = np.stack([_gain_table(p, l) for l in range(L)], 0)
    m4, mc = _masks()
    cb = _bias_tiles(np.asarray(c_rel_bias, np.float32))
    ident = np.eye(128, dtype=np.float32)
    ropeA = [_rope_tables(np.arange(r * NT, (r + 1) * NT), 64, 10000.0) for r in range(2)]
    ropeB = [_rope_tables(np.arange(r * NT, (r + 1) * NT), 16, 500000.0) for r in range(2)]
    cur = [np.ascontiguousarray(x[c // 2, (c % 2) * NT:(c % 2 + 1) * NT, :]) for c in range(ncores)]

    def common(c, sl):
        r = c % 2
        return {"x": cur[c], "w_in": w_in[sl], "w_ukv": a_w_ukv[sl], "gb": gbs[sl], "ropeA": ropeA[r],
                "ropeB": ropeB[r], "ident": ident}

    def main_extra(c, sl):
        r = c % 2
        return {"w_uq": a_w_uq[sl], "w_br": w_branch[sl], "w_out": w_out[sl], "m4": m4, "mc": mc, "cbias": cb[sl],
                "flag": np.full((128, 1), float(r), np.float32)}

    if FUSED:
        prog = _prog("fused", L, NT)
        sl = slice(0, L)
        in_maps = [dict(common(c, sl), **main_extra(c, sl)) for c in range(ncores)]
        res = run_bass_kernel_spmd(prog.nc, in_maps, core_ids=list(range(ncores)))
        cur = [np.asarray(res.results[c]["out"], np.float32) for c in range(ncores)]
    else:
        pk = _prog("kv", 1, NT)
        pm = _prog("main", 1, NT)
        for l in range(L):
            sl = slice(l, l + 1)
            res = run_bass_kernel_spmd(pk.nc, [common(c, sl) for c in range(ncores)], core_ids=list(range(ncores)))
            kx = [res.results[c]["kx_own"] for c in range(ncores)]
            vx = [res.results[c]["vx_own"] for c in range(ncores)]
            in_maps = []
            for c in range(ncores):
                d = dict(common(c, sl), **main_extra(c, sl))
                d["kx_rem"] = kx[c - c % 2]
                d["vx_rem"] = vx[c - c % 2]
                in_maps.append(d)
            res = run_bass_kernel_spmd(pm.nc, in_maps, core_ids=list(range(ncores)))
            cur = [np.asarray(res.results[c]["out"], np.float32) for c in range(ncores)]
    out = np.zeros((B, S_, D), np.float32)
    for c in range(ncores):
        out[c // 2, (c % 2) * NT:(c % 2 + 1) * NT, :] = cur[c]
    return out
```
